# Optimizing a Trainium2 kernel written in Bass

```python
import jax, jax.numpy as jnp
from jax import lax
import numpy as np

D_MODEL = 1024
BATCH = 8
SEQ = 4096
DEPTH = 1
DEC_BATCH = 16
DEC_SEQ = 4096
PAST_LEN = 128

A_HEADS = 8
A_KV_HEADS = 2
A_HEAD_DIM = 64
WINDOW = 128
BLOCK = 128
B_HEADS = 8
Q_LORA = 256
KV_LORA = 128
QK_NOPE = 64
QK_ROPE = 32
V_HEAD = 64
D_FF = 2816
CONV_W = 3
ROPE_THETA = 10000.0
EPS = 1e-6
NEG_INF = -1e30

A_Q = A_HEADS * A_HEAD_DIM
A_KV = A_KV_HEADS * A_HEAD_DIM
B_QK = QK_NOPE + QK_ROPE
IN_SPLITS = (A_Q, A_Q + A_KV, A_Q + 2 * A_KV, A_Q + 2 * A_KV + Q_LORA, A_Q + 2 * A_KV + Q_LORA + KV_LORA, A_Q + 2 * A_KV + Q_LORA + KV_LORA + QK_ROPE, A_Q + 2 * A_KV + Q_LORA + KV_LORA + QK_ROPE + D_MODEL)
IN_TOTAL = A_Q + 2 * A_KV + Q_LORA + KV_LORA + QK_ROPE + 2 * D_MODEL

kernel_name = 'hybrid_swa_mla_convffn_adaln_encoder'


def _rmsnorm(x, g):
    xf = x.astype(jnp.float32)
    y = xf * lax.rsqrt(jnp.mean(xf * xf, axis=-1, keepdims=True) + EPS)
    return y.astype(x.dtype) * g


def _rope_tables(seq, dim, dtype):
    inv = 1.0 / (ROPE_THETA ** (jnp.arange(0, dim, 2, dtype=jnp.float32) / dim))
    ang = jnp.arange(seq, dtype=jnp.float32)[:, None] * inv[None, :]
    return jnp.cos(ang).astype(dtype), jnp.sin(ang).astype(dtype)


def _apply_rope(x, cos, sin):
    c = cos[:, None, :]
    s = sin[:, None, :]
    x1, x2 = jnp.split(x, 2, axis=-1)
    return jnp.concatenate([x1 * c - x2 * s, x1 * s + x2 * c], axis=-1)


def _band(t):
    b, s, hk, d = t.shape
    nb = s // BLOCK
    tp = jnp.pad(t, ((0, 0), (BLOCK, BLOCK), (0, 0), (0, 0))).reshape(b, nb + 2, BLOCK, hk, d)
    return jnp.concatenate([tp[:, :-2], tp[:, 1:-1], tp[:, 2:]], axis=2)


def _window_attention(q, k, v, sink):
    b, s, h, dh = q.shape
    hkv = k.shape[2]
    grp = h // hkv
    nb = s // BLOCK
    qb = q.reshape(b, nb, BLOCK, hkv, grp, dh)
    kb = _band(k)
    vb = _band(v)
    sc = jnp.einsum('bnqhgd,bnkhd->bnhgqk', qb, kb).astype(jnp.float32) * (dh ** -0.5)
    qpos = jnp.arange(nb)[:, None] * BLOCK + jnp.arange(BLOCK)[None, :]
    kpos = (jnp.arange(nb)[:, None] - 1) * BLOCK + jnp.arange(3 * BLOCK)[None, :]
    valid = ((jnp.abs(qpos[:, :, None] - kpos[:, None, :]) <= WINDOW)
             & (kpos[:, None, :] >= 0) & (kpos[:, None, :] < s))
    sc = jnp.where(valid[None, :, None, None], sc, NEG_INF)
    sk = sink.astype(jnp.float32).reshape(1, 1, hkv, grp, 1, 1)
    m = jnp.maximum(jnp.max(sc, axis=-1, keepdims=True), sk)
    p = jnp.exp(sc - m)
    p = p / (jnp.sum(p, axis=-1, keepdims=True) + jnp.exp(sk - m))
    o = jnp.einsum('bnhgqk,bnkhd->bnqhgd', p.astype(v.dtype), vb)
    return o.reshape(b, s, h * dh)


def _dense_attention_blocks(q, k, v):
    b, s, h, dq = q.shape
    dv = v.shape[-1]
    nb = s // BLOCK
    qb = jnp.moveaxis(q.reshape(b, nb, BLOCK, h, dq), 1, 0)
    scale = dq ** -0.5

    def one_block(qi):
        sc = jnp.einsum('bqhd,bkhd->bhqk', qi, k).astype(jnp.float32) * scale
        p = jax.nn.softmax(sc, axis=-1)
        return jnp.einsum('bhqk,bkhd->bqhd', p.astype(v.dtype), v)

    o = lax.map(one_block, qb)
    return jnp.moveaxis(o, 0, 1).reshape(b, s, h * dv)


def _dwconv(u, w, bias):
    s = u.shape[1]
    half = CONV_W // 2
    up = jnp.pad(u, ((0, 0), (half, half), (0, 0)))
    out = bias
    for j in range(CONV_W):
        out = out + up[:, j:j + s] * w[j]
    return out


def _layer(x, c, w_ada, b_ada, g_attn, w_in, gq_a, gk_a, sink_a, g_cq, w_uq, g_ckv, w_ukv,
           gq_b, gk_b, w_o_a, w_o_b, w_out, g_ffn, w_up, conv_w, conv_b, w_down):
    b, s, _ = x.shape
    mod = jax.nn.silu(c) @ w_ada + b_ada
    sh1, sc1, gt1, sh2, sc2, gt2 = [t[:, None, :] for t in jnp.split(mod, 6, axis=-1)]

    h = _rmsnorm(x, g_attn) * (1 + sc1) + sh1
    z = h @ w_in
    qa, ka, va, cq, ckv, kr, gla, glb = jnp.split(z, IN_SPLITS, axis=-1)

    cos_a, sin_a = _rope_tables(s, A_HEAD_DIM, x.dtype)
    qa = _apply_rope(_rmsnorm(qa.reshape(b, s, A_HEADS, A_HEAD_DIM), gq_a), cos_a, sin_a)
    ka = _apply_rope(_rmsnorm(ka.reshape(b, s, A_KV_HEADS, A_HEAD_DIM), gk_a), cos_a, sin_a)
    va = va.reshape(b, s, A_KV_HEADS, A_HEAD_DIM)
    oa = _window_attention(qa, ka, va, sink_a)

    qb = (_rmsnorm(cq, g_cq) @ w_uq).reshape(b, s, B_HEADS, B_QK)
    kv = (_rmsnorm(ckv, g_ckv) @ w_ukv).reshape(b, s, B_HEADS, QK_NOPE + V_HEAD)
    k_nope = kv[..., :QK_NOPE]
    vb = kv[..., QK_NOPE:]
    kb = jnp.concatenate([k_nope, jnp.broadcast_to(kr[:, :, None, :], (b, s, B_HEADS, QK_ROPE))], axis=-1)
    qb = _rmsnorm(qb, gq_b)
    kb = _rmsnorm(kb, gk_b)
    cos_b, sin_b = _rope_tables(s, QK_ROPE, x.dtype)
    qb = jnp.concatenate([qb[..., :QK_NOPE], _apply_rope(qb[..., QK_NOPE:], cos_b, sin_b)], axis=-1)
    kb = jnp.concatenate([kb[..., :QK_NOPE], _apply_rope(kb[..., QK_NOPE:], cos_b, sin_b)], axis=-1)
    ob = _dense_attention_blocks(qb, kb, vb)

    mix = jax.nn.sigmoid(gla) * (oa @ w_o_a) + jax.nn.sigmoid(glb) * (ob @ w_o_b)
    x = x + gt1 * (mix @ w_out)

    h2 = _rmsnorm(x, g_ffn) * (1 + sc2) + sh2
    u = _dwconv(h2 @ w_up, conv_w, conv_b)
    ua, ug = jnp.split(u, 2, axis=-1)
    x = x + gt2 * ((jax.nn.silu(ug) * ua) @ w_down)
    return x


def setup_inputs(seed: int = 0) -> dict:
    key = jax.random.key(seed)
    ks = jax.random.split(key, 32)
    f32 = jnp.float32

    def nrm(k, shape, fan_in):
        return jax.random.normal(k, shape, f32) * (fan_in ** -0.5)

    def gain(k, shape):
        return 1.0 + 0.1 * jax.random.normal(k, shape, f32)

    def small(k, shape, scale):
        return scale * jax.random.normal(k, shape, f32)

    L = DEPTH
    return {
        'x_prompt': jax.random.normal(ks[0], (BATCH, SEQ, D_MODEL), f32),
        'x_sample': jax.random.normal(ks[1], (DEC_BATCH, DEC_SEQ, D_MODEL), f32),
        'c_prompt': jax.random.normal(ks[2], (BATCH, D_MODEL), f32),
        'c_sample': jax.random.normal(ks[3], (DEC_BATCH, D_MODEL), f32),
        'w_ada': nrm(ks[4], (L, D_MODEL, 6 * D_MODEL), D_MODEL),
        'b_ada': small(ks[5], (L, 6 * D_MODEL), 0.02),
        'g_attn': gain(ks[6], (L, D_MODEL)),
        'w_in': nrm(ks[7], (L, D_MODEL, IN_TOTAL), D_MODEL),
        'gq_a': gain(ks[8], (L, A_HEAD_DIM)),
        'gk_a': gain(ks[9], (L, A_HEAD_DIM)),
        'sink_a': small(ks[10], (L, A_HEADS), 0.5),
        'g_cq': gain(ks[11], (L, Q_LORA)),
        'w_uq': nrm(ks[12], (L, Q_LORA, B_HEADS * B_QK), Q_LORA),
        'g_ckv': gain(ks[13], (L, KV_LORA)),
        'w_ukv': nrm(ks[14], (L, KV_LORA, B_HEADS * (QK_NOPE + V_HEAD)), KV_LORA),
        'gq_b': gain(ks[15], (L, B_QK)),
        'gk_b': gain(ks[16], (L, B_QK)),
        'w_o_a': nrm(ks[17], (L, A_Q, D_MODEL), A_Q),
        'w_o_b': nrm(ks[18], (L, B_HEADS * V_HEAD, D_MODEL), B_HEADS * V_HEAD),
        'w_out': nrm(ks[19], (L, D_MODEL, D_MODEL), D_MODEL),
        'g_ffn': gain(ks[20], (L, D_MODEL)),
        'w_up': nrm(ks[21], (L, D_MODEL, 2 * D_FF), D_MODEL),
        'conv_w': nrm(ks[22], (L, CONV_W, 2 * D_FF), CONV_W),
        'conv_b': small(ks[23], (L, 2 * D_FF), 0.02),
        'w_down': nrm(ks[24], (L, D_FF, D_MODEL), D_FF),
    }


def reference(x_prompt, x_sample, c_prompt, c_sample, w_ada, b_ada, g_attn, w_in, gq_a, gk_a,
              sink_a, g_cq, w_uq, g_ckv, w_ukv, gq_b, gk_b, w_o_a, w_o_b, w_out, g_ffn, w_up,
              conv_w, conv_b, w_down):
    y_prompt = x_prompt
    y_sample = x_sample
    for l in range(DEPTH):
        p = (w_ada[l], b_ada[l], g_attn[l], w_in[l], gq_a[l], gk_a[l], sink_a[l], g_cq[l], w_uq[l],
             g_ckv[l], w_ukv[l], gq_b[l], gk_b[l], w_o_a[l], w_o_b[l], w_out[l], g_ffn[l], w_up[l],
             conv_w[l], conv_b[l], w_down[l])
        y_prompt = _layer(y_prompt, c_prompt, *p)
        y_sample = _layer(y_sample, c_sample, *p)
    return (y_prompt, y_sample)
```

```python
import os
import numpy as np
from contextlib import ExitStack
import concourse.bass as bass
import concourse.mybir as mybir
from concourse.alu_op_type import AluOpType as ALU
from concourse.bass_utils import run_bass_kernel_spmd

F32 = mybir.dt.float32
BF16 = mybir.dt.bfloat16
AF = mybir.ActivationFunctionType
AX = mybir.AxisListType
ENGS = ['pe', 'act', 'dve', 'pool', 'sp']
SAME_ENGINE_SYNC = True
SEM_ROT = 60000

D = 1024
KC = 8
EPS = 1e-6
DFF = 2816
NFF = 22
FT = 256
N_CORES = 8
DBG = int(os.environ.get('KDBG', '99'))
DBG2 = int(os.environ.get('KDBG2', '99'))


class Buf:
    __slots__ = ('name', 'w', 'r')

    def __init__(self, name=''):
        self.name = name
        self.w = None
        self.r = []


class DSem:
    n = 0

    def __init__(self, ctx, name):
        DSem.n += 1
        self.uid = DSem.n
        self.sem = ctx.new_sem(name)
        self.count = 0


class Ctx:
    def __init__(self, nc, stack):
        self.nc = nc
        self.stack = stack
        self.prog = {e: [] for e in ENGS}
        self.count = {e: 0 for e in ENGS}
        self.seen = {e: {} for e in ENGS}
        self.esems = {e: [] for e in ENGS}
        self.last = {e: None for e in ENGS}
        self.dsems = []

    def new_sem(self, name):
        return self.stack.enter_context(self.nc.semaphore(name))

    def dsem(self, name):
        d = DSem(self, name)
        self.dsems.append(d)
        return d

    def _esem(self, eng, idx):
        while len(self.esems[eng]) <= idx:
            self.esems[eng].append(self.new_sem(f"e_{eng}_{len(self.esems[eng])}"))
        return self.esems[eng][idx]

    def wait(self, eng, tok):
        if tok is None:
            return
        sem, key, val = tok
        if not SAME_ENGINE_SYNC and key[0] == eng:
            return
        if self.seen[eng].get(key, 0) >= val:
            return
        self.seen[eng][key] = val
        self.prog[eng].append(lambda e, sem=sem, val=val: e.wait_ge(sem, val))

    def op(self, eng, fn, deps=()):
        for t in deps:
            self.wait(eng, t)
        c = self.count[eng]
        self.count[eng] += 1
        idx, v = divmod(c, SEM_ROT)
        sem = self._esem(eng, idx)
        self.prog[eng].append(lambda e, fn=fn, sem=sem: fn(e).then_inc(sem, 1))
        tok = (sem, (eng, idx), v + 1)
        self.last[eng] = tok
        return tok

    @staticmethod
    def _deps(reads, writes, extra):
        deps = list(extra)
        for b in reads:
            if b.w is not None:
                deps.append(b.w)
        for b in writes:
            if b.w is not None:
                deps.append(b.w)
            deps += b.r
        return deps

    @staticmethod
    def _reg(tok, reads, writes):
        for b in reads:
            b.r.append(tok)
            if len(b.r) > 24:
                b.r = b.r[-24:] if False else b.r
        for b in writes:
            b.w = tok
            b.r = []

    def use(self, eng, fn, reads=(), writes=(), extra=()):
        tok = self.op(eng, fn, self._deps(reads, writes, extra))
        self._reg(tok, reads, writes)
        return tok

    def group(self, eng, fns, reads=(), writes=(), extra=()):
        for t in self._deps(reads, writes, extra):
            self.wait(eng, t)
        for f in fns[:-1]:
            self.prog[eng].append(f)
        tok = self.op(eng, fns[-1], ())
        self._reg(tok, reads, writes)
        return tok

    def dma_group(self, q, pairs, dsem, reads=(), writes=(), extra=(), **kw):
        for t in self._deps(reads, writes, extra):
            self.wait(q, t)
        for out, in_ in pairs:
            dsem.count += 16
            self.prog[q].append(lambda e, out=out, in_=in_, sem=dsem.sem, kw=kw:
                                e.dma_start(out=out, in_=in_, **kw).then_inc(sem, 16))
        tok = (dsem.sem, ('dma', dsem.uid), dsem.count)
        self._reg(tok, reads, writes)
        return tok

    def barrier(self):
        toks = [self.last[e] for e in ENGS if self.last[e] is not None]
        toks += [(d.sem, ('dma', d.uid), d.count) for d in self.dsems if d.count > 0]
        for e in ENGS:
            for t in toks:
                self.wait(e, t)

    def emit(self):
        nc = self.nc
        with nc.Block() as block:
            @block.sync
            def _(e):
                for f in self.prog['sp']:
                    f(e)

            @block.tensor
            def _(e):
                for f in self.prog['pe']:
                    f(e)

            @block.scalar
            def _(e):
                for f in self.prog['act']:
                    f(e)

            @block.vector
            def _(e):
                for f in self.prog['dve']:
                    f(e)

            @block.gpsimd
            def _(e):
                for f in self.prog['pool']:
                    f(e)


class NS:
    pass


def run_if(cond):
    def deco(f):
        if cond:
            f()
        return f
    return deco


def ffn_tiles(S):
    tiles = []
    s = 0
    while s < S:
        lo = max(s - 1, 0)
        hi = min(lo + FT, S)
        e = hi - 1 if hi < S else S
        tiles.append((lo, hi, s, e))
        s = e
    return tiles


def build_program(nseq, S, phases=('p1', 'p2', 'p3a', 'p3b')):
    NB = S // 128
    nc = bass.Bass("TRN2", target_bir_lowering=False)

    def din(name, shape):
        return nc.dram_tensor(name, shape, F32, kind="ExternalInput").ap()

    x = din("x", [nseq, S, D])
    cT = din("cT", [128, KC * nseq])
    w_ada = din("w_ada", [D, 6 * D])
    b_ada_rep = din("b_ada_rep", [nseq, 6 * D])
    g_attn_c = din("g_attn_c", [128, KC])
    g_ffn_c = din("g_ffn_c", [128, KC])
    w_in = din("w_in", [D, 3232])
    gA_d = din("gA", [128, 640])
    gqb_d = din("gqb", [128, 96])
    gkb_d = din("gkb", [128, 96])
    sink_d = din("sink_rep", [128, 8])
    g_cq_c = din("g_cq_c", [128, 2])
    g_ckv_c = din("g_ckv_c", [128, 1])
    w_uq = din("w_uq", [256, 768])
    w_ukv = din("w_ukv", [128, 1024])
    w_o_a = din("w_o_a", [512, D])
    w_o_b = din("w_o_b", [512, D])
    w_out = din("w_out", [D, D])
    w_up = din("w_up", [D, 2 * DFF])
    w_down = din("w_down", [DFF, D])
    convw_d = din("convw_c", [128, 3 * 44])
    convb_d = din("convb_c", [128, 44])
    ident_d = din("ident", [128, 128])
    maskL_d = din("maskL", [128, 128])
    maskR_d = din("maskR", [128, 128])
    cosA_d = din("cosA", [128, NB * 32])
    sinA_d = din("sinA", [128, NB * 32])
    cosB_d = din("cosB", [128, NB * 16])
    sinB_d = din("sinB", [128, NB * 16])
    sel_d = din("sel", [nseq, nseq * 128])
    y = nc.dram_tensor("y", [nseq, S, D], F32, kind="ExternalOutput").ap()
    x1d = (nc.dram_tensor("x1_scratch", [nseq, S, D], F32, kind="Internal").ap() if 'p3b' in phases else y)

    with ExitStack() as top:
        k = Ctx(nc, top)

        uniq = [0]

        def sbt(stack, name, shape, dt=F32):
            uniq[0] += 1
            return stack.enter_context(nc.sbuf_tensor(f"{name}_u{uniq[0]}", shape, dt))

        PF = [top.enter_context(nc.psum_tensor(f"pf{i}", [128, 512], F32)) for i in range(6)]
        PT_ = [top.enter_context(nc.psum_tensor(f"pt{i}", [128, 1024], BF16)) for i in range(2)]
        BPF = [Buf(f"pf{i}") for i in range(6)]
        BPT = [Buf(f"pt{i}") for i in range(2)]

        P = NS()
        P.identf = sbt(top, "identf", [128, 128]); P.identb = sbt(top, "identb", [128, 128], BF16)
        P.mLf = sbt(top, "mLf", [128, 128]); P.mRf = sbt(top, "mRf", [128, 128])
        P.maskL = sbt(top, "maskL_b", [128, 128], BF16); P.maskR = sbt(top, "maskR_b", [128, 128], BF16)
        P.gA = sbt(top, "gA_s", [128, 640]); P.gqb = sbt(top, "gqb_s", [128, 96]); P.gkb = sbt(top, "gkb_s", [128, 96])
        P.esink = sbt(top, "esink", [128, 8])
        P.mh = sbt(top, "mh", [128, 16])
        P.modT = sbt(top, "modT", [128, 48 * nseq])
        P.convw = sbt(top, "convw_s", [128, 3 * 44]); P.convb = sbt(top, "convb_s", [128, 44])
        P.gattn = sbt(top, "gattn_s", [128, KC]); P.gffn = sbt(top, "gffn_s", [128, KC])
        P.gcq = sbt(top, "gcq_s", [128, 2]); P.gckv = sbt(top, "gckv_s", [128, 1])
        P.mgt = sbt(top, "mgt", [nseq, 2 * D])
        P.gtbc = sbt(top, "gtbc", [128, 2 * D])
        P.sel = sbt(top, "sel_s", [nseq, nseq * 128])
        P.a1 = sbt(top, "a1", [128, KC]); P.s1 = sbt(top, "s1", [128, KC])
        P.a2 = sbt(top, "a2", [128, KC]); P.s2 = sbt(top, "s2", [128, KC])
        P.wuq = sbt(top, "wuq_b", [128, 2, 768], BF16); P.wukv = sbt(top, "wukv_b", [128, 1024], BF16)
        P.ssq = sbt(top, "ssq", [128, 4]); P.rstd = sbt(top, "rstd", [128, 4])
        BC = Buf("consts")
        Bmod = Buf("modcols")
        Bgt = Buf("gtbc")

        cld = k.dsem("cld")
        k.dma_group('sp', [
            (P.identf[:], ident_d[:, :]), (P.mLf[:], maskL_d[:, :]), (P.mRf[:], maskR_d[:, :]),
            (P.gA[:], gA_d[:, :]), (P.gqb[:], gqb_d[:, :]), (P.gkb[:], gkb_d[:, :]), (P.esink[:], sink_d[:, :]),
            (P.convw[:], convw_d[:, :]), (P.convb[:], convb_d[:, :]), (P.gattn[:], g_attn_c[:, :]),
            (P.gffn[:], g_ffn_c[:, :]), (P.gcq[:], g_cq_c[:, :]), (P.gckv[:], g_ckv_c[:, :]), (P.sel[:], sel_d[:, :]),
        ], cld, writes=[BC])
        k.use('dve', lambda e: e.tensor_copy(out=P.identb[:], in_=P.identf[:]), reads=[BC], writes=[BC])
        k.use('dve', lambda e: e.tensor_copy(out=P.maskL[:], in_=P.mLf[:]), reads=[BC], writes=[BC])
        k.use('dve', lambda e: e.tensor_copy(out=P.maskR[:], in_=P.mRf[:]), reads=[BC], writes=[BC])
        k.use('dve', lambda e: e.tensor_scalar(out=P.gA[:, 0:512], in0=P.gA[:, 0:512], scalar1=0.125, scalar2=None,
                                               op0=ALU.mult), reads=[BC], writes=[BC])
        k.use('dve', lambda e: e.tensor_scalar(out=P.gqb[:], in0=P.gqb[:], scalar1=float(96 ** -0.5), scalar2=None,
                                               op0=ALU.mult), reads=[BC], writes=[BC])
        k.use('act', lambda e: e.activation(out=P.esink[:], in_=P.esink[:], func=AF.Exp), reads=[BC], writes=[BC])
        k.use('pool', lambda e: e.memset(P.mh[:], -0.5), reads=[BC], writes=[BC])

        with ExitStack() as st0:
            wuq_f = sbt(st0, "wuq_f", [128, 2, 768]); wukv_f = sbt(st0, "wukv_f", [128, 1024])
            ct = sbt(st0, "ct", [128, KC * nseq]); tct = sbt(st0, "tct", [128, KC * nseq]); sct = sbt(st0, "sct", [128, KC * nseq])
            brep = sbt(st0, "brep", [nseq, 6 * D]); modrow = sbt(st0, "modrow", [nseq, 6 * D])
            wa = [sbt(st0, f"wa{i}", [128, KC, 512]) for i in range(2)]
            Bw0 = Buf(); Bct = Buf(); Bmr = Buf(); Bwa = [Buf(), Buf()]
            l0 = k.dsem("l0")
            k.dma_group('sp', [(wuq_f[:, 0, :], w_uq[0:128, :]), (wuq_f[:, 1, :], w_uq[128:256, :]),
                               (wukv_f[:], w_ukv[:, :]), (ct[:], cT[:, :]), (brep[:], b_ada_rep[:, :])], l0, writes=[Bw0, Bct])
            for c in range(2):
                k.use('dve', lambda e, c=c: e.tensor_scalar(out=P.wuq[:, c, :], in0=wuq_f[:, c, :], scalar1=P.gcq[:, c:c + 1],
                                                            scalar2=None, op0=ALU.mult), reads=[Bw0, BC], writes=[BC])
            k.use('dve', lambda e: e.tensor_scalar(out=P.wukv[:], in0=wukv_f[:], scalar1=P.gckv[:, 0:1], scalar2=None,
                                                   op0=ALU.mult), reads=[Bw0, BC], writes=[BC])
            k.use('act', lambda e: e.activation(out=tct[:], in_=ct[:], func=AF.Tanh, scale=0.5), reads=[Bct], writes=[Bct])
            k.use('dve', lambda e: e.scalar_tensor_tensor(out=sct[:], in0=tct[:], scalar=1.0, in1=ct[:], op0=ALU.add,
                                                          op1=ALU.mult), reads=[Bct], writes=[Bct])
            k.use('dve', lambda e: e.tensor_scalar(out=sct[:], in0=sct[:], scalar1=0.5, scalar2=None, op0=ALU.mult),
                  reads=[Bct], writes=[Bct])
            wsem = [k.dsem("wa0"), k.dsem("wa1")]
            for cg in range(12):
                sl = cg % 2
                k.dma_group('sp', [(wa[sl][:, kc, :], w_ada[kc * 128:(kc + 1) * 128, cg * 512:(cg + 1) * 512]) for kc in range(KC)],
                            wsem[sl], writes=[Bwa[sl]])
                pf = PF[cg % 2]
                k.group('pe', [lambda e, kc=kc, sl=sl, pf=pf: e.matmul(pf[0:nseq, :], lhsT=sct[:, kc * nseq:(kc + 1) * nseq],
                                                                      rhs=wa[sl][:, kc, :], start=(kc == 0), stop=(kc == KC - 1))
                               for kc in range(KC)], reads=[Bwa[sl], Bct], writes=[BPF[cg % 2]])
                k.use('dve', lambda e, cg=cg, pf=pf: e.tensor_tensor(out=modrow[:, cg * 512:(cg + 1) * 512], in0=pf[0:nseq, :],
                                                                     in1=brep[:, cg * 512:(cg + 1) * 512], op=ALU.add),
                      reads=[BPF[cg % 2], Bct], writes=[Bmr])
            k.group('pe', [lambda e, m=m: e.transpose(out=PF[2][:, m * nseq:(m + 1) * nseq], in_=modrow[0:nseq, m * 128:(m + 1) * 128],
                                                      identity=P.identf[0:nseq, 0:nseq]) for m in range(48)],
                    reads=[Bmr, BC], writes=[BPF[2]])
            k.use('dve', lambda e: e.tensor_copy(out=P.modT[:], in_=PF[2][:, 0:48 * nseq]), reads=[BPF[2]], writes=[Bmod])
            k.use('dve', lambda e: e.tensor_scalar(out=P.mgt[:, 0:D], in0=modrow[:, 2 * D:3 * D], scalar1=0.5, scalar2=None,
                                                   op0=ALU.mult), reads=[Bmr], writes=[Bmod])
            k.use('dve', lambda e: e.tensor_scalar(out=P.mgt[:, D:2 * D], in0=modrow[:, 5 * D:6 * D], scalar1=0.5, scalar2=None,
                                                   op0=ALU.mult), reads=[Bmr], writes=[Bmod])
            k.barrier()

        modT3 = P.modT[:].rearrange("p (m s) -> p m s", s=nseq)

        def prep(xt, Bxt, nr, a_col, s_col, hT, BhT, c0, W):
            k.use('act', lambda e: e.activation(out=W.junk[0:nr, :], in_=xt[0:nr, :], func=AF.Square, accum_out=P.ssq[0:nr, 0:1]),
                  reads=[Bxt], writes=[W.Bjunk, W.Bst])
            k.use('dve', lambda e: e.tensor_scalar(out=P.ssq[0:nr, 0:1], in0=P.ssq[0:nr, 0:1], scalar1=1.0 / D, scalar2=EPS,
                                                   op0=ALU.mult, op1=ALU.add), reads=[W.Bst], writes=[W.Bst])
            k.use('pool', lambda e: e.tensor_tensor(out=P.rstd[0:nr, 0:1], in0=P.ssq[0:nr, 0:1], in1=P.mh[0:nr, 0:1], op=ALU.pow),
                  reads=[W.Bst, BC], writes=[W.Bst])
            k.use('dve', lambda e: e.tensor_scalar(out=W.xn[0:nr, :], in0=xt[0:nr, :], scalar1=P.rstd[0:nr, 0:1], scalar2=None,
                                                   op0=ALU.mult), reads=[Bxt, W.Bst], writes=[W.Bxn])
            k.group('pe', [lambda e, c=c: e.transpose(out=PT_[0][:, c * 128:c * 128 + nr], in_=W.xn[0:nr, c * 128:(c + 1) * 128],
                                                      identity=P.identb[0:nr, 0:nr]) for c in range(KC)],
                    reads=[W.Bxn, BC], writes=[BPT[0]])
            for c in range(KC):
                if c % 2 == 0:
                    k.use('act', lambda e, c=c: e.activation(out=hT[:, c, c0:c0 + nr], in_=PT_[0][:, c * 128:c * 128 + nr],
                                                             func=AF.Identity, scale=a_col[:, c:c + 1], bias=s_col[:, c:c + 1]),
                          reads=[BPT[0], Bmod], writes=[BhT])
                else:
                    k.use('dve', lambda e, c=c: e.tensor_scalar(out=hT[:, c, c0:c0 + nr], in0=PT_[0][:, c * 128:c * 128 + nr],
                                                                scalar1=a_col[:, c:c + 1], scalar2=s_col[:, c:c + 1],
                                                                op0=ALU.mult, op1=ALU.add), reads=[BPT[0], Bmod], writes=[BhT])

        def normrope(src, src_bufs, H, Dh, r0, R, gain, cos, sin, out, Bout, T):
            HD = H * Dh

            def v3(t, d=Dh):
                return t[:, 0:H * d].rearrange("p (h d) -> p h d", h=H)
            sq, u, n = v3(T.sq), v3(T.u), v3(T.n)
            t1, t2, t3, t4 = v3(T.t1, R), v3(T.t2, R), v3(T.t3, R), v3(T.t4, R)
            k.use('act', lambda e: e.activation(out=sq, in_=src, func=AF.Square), reads=src_bufs, writes=[T.Bsq])
            k.use('dve', lambda e: e.tensor_reduce(out=T.ss[:, 0:H], in_=sq, axis=AX.X, op=ALU.add), reads=[T.Bsq], writes=[T.Bss])
            k.use('dve', lambda e: e.tensor_scalar(out=T.ss[:, 0:H], in0=T.ss[:, 0:H], scalar1=1.0 / Dh, scalar2=EPS,
                                                   op0=ALU.mult, op1=ALU.add), reads=[T.Bss], writes=[T.Bss])
            k.use('pool', lambda e: e.tensor_tensor(out=T.rs[:, 0:H], in0=T.ss[:, 0:H], in1=P.mh[:, 0:H], op=ALU.pow),
                  reads=[T.Bss, BC], writes=[T.Brs])
            k.use('dve', lambda e: e.tensor_tensor(out=u, in0=src, in1=T.rs[:, 0:H].unsqueeze(2).broadcast_to([128, H, Dh]),
                                                   op=ALU.mult), reads=list(src_bufs) + [T.Brs], writes=[T.Bu])
            k.use('pool', lambda e: e.tensor_tensor(out=n, in0=u, in1=gain, op=ALU.mult), reads=[T.Bu, BC], writes=[T.Bn])
            x1 = n[:, :, r0:r0 + R]
            x2 = n[:, :, r0 + R:r0 + 2 * R]
            cb = cos.unsqueeze(1).broadcast_to([128, H, R])
            sb_ = sin.unsqueeze(1).broadcast_to([128, H, R])
            k.use('dve', lambda e: e.tensor_tensor(out=t1, in0=x1, in1=cb, op=ALU.mult), reads=[T.Bn, T.Bcs], writes=[T.Bt1])
            k.use('pool', lambda e: e.tensor_tensor(out=t2, in0=x2, in1=sb_, op=ALU.mult), reads=[T.Bn, T.Bcs], writes=[T.Bt2])
            k.use('dve', lambda e: e.tensor_tensor(out=t3, in0=x1, in1=sb_, op=ALU.mult), reads=[T.Bn, T.Bcs], writes=[T.Bt3])
            k.use('pool', lambda e: e.tensor_tensor(out=t4, in0=x2, in1=cb, op=ALU.mult), reads=[T.Bn, T.Bcs], writes=[T.Bt4])
            k.use('dve', lambda e: e.tensor_tensor(out=out[:, :, r0:r0 + R], in0=t1, in1=t2, op=ALU.subtract),
                  reads=[T.Bt1, T.Bt2], writes=[Bout])
            k.use('pool', lambda e: e.tensor_tensor(out=out[:, :, r0 + R:r0 + 2 * R], in0=t3, in1=t4, op=ALU.add),
                  reads=[T.Bt3, T.Bt4], writes=[Bout])
            if r0 > 0:
                k.use('act', lambda e: e.activation(out=out[:, :, 0:r0], in_=n[:, :, 0:r0], func=AF.Copy), reads=[T.Bn], writes=[Bout])

        def mk_nr_tmp(stack, tag, width, rw):
            T = NS()
            for nm in ('sq', 'u', 'n'):
                setattr(T, nm, sbt(stack, f"{tag}_{nm}", [128, width]))
                setattr(T, 'B' + nm, Buf())
            for nm in ('t1', 't2', 't3', 't4'):
                setattr(T, nm, sbt(stack, f"{tag}_{nm}", [128, rw]))
                setattr(T, 'B' + nm, Buf())
            T.ss = sbt(stack, f"{tag}_ss", [128, 16]); T.rs = sbt(stack, f"{tag}_rs", [128, 16])
            T.Bss = Buf(); T.Brs = Buf(); T.Bcs = Buf()
            return T

        def load_w(q, dst, src, nk, c0, cols, dsem, buf):
            pairs = []
            for kc in range(nk):
                for cg in range(0, cols, 1024):
                    w = min(1024, cols - cg)
                    pairs.append((dst[:, kc, cg:cg + w], src[kc * 128:(kc + 1) * 128, c0 + cg:c0 + cg + w]))
            k.dma_group(q, pairs, dsem, writes=[buf])

        def do_seq(s):
            k.use('dve', lambda e, s=s: e.scalar_tensor_tensor(out=P.a1[:], in0=modT3[:, 8:16, s], scalar=1.0, in1=P.gattn[:],
                                                               op0=ALU.add, op1=ALU.mult), reads=[Bmod, BC], writes=[Bmod])
            k.use('dve', lambda e, s=s: e.tensor_copy(out=P.s1[:], in_=modT3[:, 0:8, s]), reads=[Bmod], writes=[Bmod])
            k.use('dve', lambda e, s=s: e.scalar_tensor_tensor(out=P.a2[:], in0=modT3[:, 32:40, s], scalar=1.0, in1=P.gffn[:],
                                                               op0=ALU.add, op1=ALU.mult), reads=[Bmod, BC], writes=[Bmod])
            k.use('dve', lambda e, s=s: e.tensor_copy(out=P.s2[:], in_=modT3[:, 24:32, s]), reads=[Bmod], writes=[Bmod])
            for q4 in range(4):
                pf = PF[q4 % 2]
                k.use('pe', lambda e, q4=q4, pf=pf, s=s: e.matmul(pf[:, :], lhsT=P.sel[0:nseq, s * 128:(s + 1) * 128],
                                                                 rhs=P.mgt[0:nseq, q4 * 512:(q4 + 1) * 512], start=True, stop=True),
                      reads=[Bmod, BC], writes=[BPF[q4 % 2]])
                k.use('dve', lambda e, q4=q4, pf=pf: e.tensor_copy(out=P.gtbc[:, q4 * 512:(q4 + 1) * 512], in_=pf[:, :]),
                      reads=[BPF[q4 % 2]], writes=[Bgt])

            with ExitStack() as sq_:
                oaT = sbt(sq_, "oaT", [128, 4, S], BF16); obT = sbt(sq_, "obT", [128, 4, S], BF16)
                BoaT = Buf(); BobT = Buf()
                with ExitStack() as at_:
                    cqnT = sbt(at_, "cqnT", [128, 2, S], BF16); ckvnT = sbt(at_, "ckvnT", [128, S], BF16)
                    krT = sbt(at_, "krTM", [128, NB, 32])
                    cosA = sbt(at_, "cosA_s", [128, NB, 32]); sinA = sbt(at_, "sinA_s", [128, NB, 32])
                    cosB = sbt(at_, "cosB_s", [128, NB, 16]); sinB = sbt(at_, "sinB_s", [128, NB, 16])
                    Blat = Buf(); Bkr = Buf(); Brt = Buf()
                    rl = k.dsem(f"rl{s}")
                    k.dma_group('sp', [(cosA[:].rearrange("p a b -> p (a b)"), cosA_d[:, :]),
                                       (sinA[:].rearrange("p a b -> p (a b)"), sinA_d[:, :]),
                                       (cosB[:].rearrange("p a b -> p (a b)"), cosB_d[:, :]),
                                       (sinB[:].rearrange("p a b -> p (a b)"), sinB_d[:, :])], rl, writes=[Brt])

                    @run_if('p1' in phases)
                    def _p1():
                        with ExitStack() as p1:
                            winA = sbt(p1, "winA", [128, KC, 1184], BF16); BwinA = Buf()
                            ws = k.dsem(f"ws1_{s}")
                            load_w('pool', winA, w_in, KC, 0, 1184, ws, BwinA)
                            W = NS()
                            W.junk = sbt(p1, "junk", [128, D], BF16); W.xn = sbt(p1, "xn", [128, D], BF16)
                            W.Bjunk = Buf(); W.Bst = Buf(); W.Bxn = Buf()
                            xts = [sbt(p1, f"xt{i}", [128, D]) for i in range(2)]; Bxts = [Buf(), Buf()]
                            xsem = [k.dsem(f"x1_{s}_{i}") for i in range(2)]
                            hTs = [sbt(p1, f"hT{i}", [128, KC, 128], BF16) for i in range(2)]; BhTs = [Buf(), Buf()]
                            TA = mk_nr_tmp(p1, "nrA", 512, 256)
                            TA.Bcs = Brt
                            qb_ = sbt(p1, "qa_bf", [128, 8, 64], BF16); kb_ = sbt(p1, "ka_bf", [128, 2, 64], BF16)
                            Bqb = Buf(); Bkb = Buf()
                            QT = [sbt(p1, f"QT{i}", [64, 8, 128], BF16) for i in range(2)]; BQT = [Buf(), Buf()]
                            KT = [sbt(p1, f"KT{i}", [64, 2, 128], BF16) for i in range(4)]; BKT = [Buf() for _ in range(4)]
                            VE = [sbt(p1, f"VE{i}", [128, 2, 128], BF16) for i in range(4)]; BVE = [Buf() for _ in range(4)]
                            PTs = [sbt(p1, f"PTa{i}", [128, 512], BF16) for i in range(3)]; BPTs = [Buf() for _ in range(3)]
                            lat = sbt(p1, "lat_bf", [128, 384], BF16); Blt = Buf()
                            dn = sbt(p1, "dn", [128, 512]); rd = sbt(p1, "rd", [128, 512]); Bdn = Buf(); Brd = Buf()
                            for i in range(4):
                                k.use('pool', lambda e, i=i: e.memset(VE[i][:, :, 64:128], 1.0), writes=[BVE[i]])
                            ptc = [0]

                            def win_attn(i):
                                for g in range(2):
                                    js = [j for j in (i - 1, i, i + 1) if 0 <= j < NB]
                                    for ji, j in enumerate(js):
                                        sp_ = 3 + (ptc[0] % 2) * 2
                                        pt = ptc[0] % 3
                                        ptc[0] += 1
                                        k.use('pe', lambda e, j=j, g=g, i=i, sp_=sp_: e.matmul(
                                            PF[sp_][:, :], lhsT=KT[j % 4][:, g, :],
                                            rhs=QT[i % 2][:, 4 * g:4 * g + 4, :].rearrange("p a b -> p (a b)"), start=True, stop=True),
                                            reads=[BKT[j % 4], BQT[i % 2]], writes=[BPF[sp_]])
                                        k.use('act', lambda e, sp_=sp_, pt=pt: e.activation(out=PTs[pt][:], in_=PF[sp_][:, :], func=AF.Exp),
                                              reads=[BPF[sp_]], writes=[BPTs[pt]])
                                        if j != i and DBG >= 6:
                                            mk = P.maskL if j < i else P.maskR
                                            k.use('pool', lambda e, pt=pt, mk=mk: e.tensor_tensor(
                                                out=PTs[pt][:].rearrange("p (a b) -> p a b", a=4),
                                                in0=PTs[pt][:].rearrange("p (a b) -> p a b", a=4),
                                                in1=mk[:].unsqueeze(1).broadcast_to([128, 4, 128]), op=ALU.mult),
                                                reads=[BPTs[pt], BC], writes=[BPTs[pt]])
                                        if DBG >= 7:
                                            k.use('pe', lambda e, j=j, g=g, pt=pt, ji=ji, n=len(js): e.matmul(
                                                PF[4][:, :], lhsT=VE[j % 4][:, g, :], rhs=PTs[pt][:], start=(ji == 0), stop=(ji == n - 1)),
                                                reads=[BVE[j % 4], BPTs[pt]], writes=[BPF[4]])
                                    if DBG < 8:
                                        continue
                                    k.use('dve', lambda e, g=g: e.tensor_tensor(
                                        out=dn[64:128, :].rearrange("p (a b) -> p a b", a=4),
                                        in0=PF[4][64:128, :].rearrange("p (a b) -> p a b", a=4),
                                        in1=P.esink[64:128, 4 * g:4 * g + 4].unsqueeze(2).broadcast_to([64, 4, 128]), op=ALU.add),
                                        reads=[BPF[4], BC], writes=[Bdn])
                                    k.use('dve', lambda e: e.reciprocal(out=rd[64:128, :], in_=dn[64:128, :]), reads=[Bdn], writes=[Brd])
                                    if DBG < 9:
                                        continue
                                    for hh in range(4):
                                        h = 4 * g + hh
                                        k.use('dve', lambda e, h=h, hh=hh, i=i: e.tensor_tensor(
                                            out=oaT[(h % 2) * 64:(h % 2) * 64 + 64, h // 2, i * 128:(i + 1) * 128],
                                            in0=PF[4][0:64, hh * 128:(hh + 1) * 128],
                                            in1=rd[64:128, hh * 128:(hh + 1) * 128], op=ALU.mult),
                                            reads=[BPF[4], Brd], writes=[BoaT])

                            for b in range(NB):
                                sl = b % 2
                                k.dma_group('sp', [(xts[sl][:], x[s, b * 128:(b + 1) * 128, :])], xsem[sl], writes=[Bxts[sl]])
                                prep(xts[sl], Bxts[sl], 128, P.a1, P.s1, hTs[sl], BhTs[sl], 0, W)
                                if DBG < 1:
                                    continue
                                for gi, (c0, cw) in enumerate(((0, 512), (512, 512), (1024, 160))):
                                    k.group('pe', [lambda e, kc=kc, c0=c0, cw=cw, gi=gi, sl=sl: e.matmul(
                                        PF[gi][:, 0:cw], lhsT=hTs[sl][:, kc, :], rhs=winA[:, kc, c0:c0 + cw],
                                        start=(kc == 0), stop=(kc == KC - 1)) for kc in range(KC)],
                                        reads=[BhTs[sl], BwinA], writes=[BPF[gi]])
                                if DBG < 2:
                                    continue
                                gq3 = P.gA[:, 0:512].rearrange("p (h d) -> p h d", h=8)
                                gk3 = P.gA[:, 512:640].rearrange("p (h d) -> p h d", h=2)
                                normrope(PF[0][:, :].rearrange("p (h d) -> p h d", h=8), [BPF[0]], 8, 64, 0, 32, gq3,
                                         cosA[:, b, :], sinA[:, b, :], qb_[:], Bqb, TA)
                                normrope(PF[1][:, 0:128].rearrange("p (h d) -> p h d", h=2), [BPF[1]], 2, 64, 0, 32, gk3,
                                         cosA[:, b, :], sinA[:, b, :], kb_[:], Bkb, TA)
                                k.use('act', lambda e, b=b: e.activation(out=VE[b % 4][:, :, 0:64],
                                                                         in_=PF[1][:, 128:256].rearrange("p (h d) -> p h d", h=2),
                                                                         func=AF.Copy), reads=[BPF[1]], writes=[BVE[b % 4]])
                                if DBG < 3:
                                    continue
                                k.group('pe', [lambda e, h=h: e.transpose(out=PT_[1][0:64, h * 128:(h + 1) * 128], in_=qb_[:, h, :],
                                                                          identity=P.identb[:]) for h in range(8)],
                                        reads=[Bqb, BC], writes=[BPT[1]])
                                k.use('act', lambda e, b=b: e.activation(out=QT[b % 2][:].rearrange("p a b -> p (a b)"),
                                                                         in_=PT_[1][0:64, 0:1024], func=AF.Copy),
                                      reads=[BPT[1]], writes=[BQT[b % 2]])
                                k.group('pe', [lambda e, h=h: e.transpose(out=PT_[1][0:64, h * 128:(h + 1) * 128], in_=kb_[:, h, :],
                                                                          identity=P.identb[:]) for h in range(2)],
                                        reads=[Bkb, BC], writes=[BPT[1]])
                                k.use('act', lambda e, b=b: e.activation(out=KT[b % 4][:].rearrange("p a b -> p (a b)"),
                                                                         in_=PT_[1][0:64, 0:256], func=AF.Copy), reads=[BPT[1]], writes=[BKT[b % 4]])
                                if DBG < 4:
                                    continue
                                k.use('act', lambda e: e.activation(out=W.junk[:, 0:256], in_=PF[1][:, 256:512], func=AF.Square,
                                                                    accum_out=P.ssq[:, 1:2]), reads=[BPF[1]], writes=[W.Bjunk, W.Bst])
                                k.use('act', lambda e: e.activation(out=W.junk[:, 0:128], in_=PF[2][:, 0:128], func=AF.Square,
                                                                    accum_out=P.ssq[:, 2:3]), reads=[BPF[2]], writes=[W.Bjunk, W.Bst])
                                k.use('dve', lambda e: e.tensor_scalar(out=P.ssq[:, 1:2], in0=P.ssq[:, 1:2], scalar1=1.0 / 256, scalar2=EPS,
                                                                       op0=ALU.mult, op1=ALU.add), reads=[W.Bst], writes=[W.Bst])
                                k.use('dve', lambda e: e.tensor_scalar(out=P.ssq[:, 2:3], in0=P.ssq[:, 2:3], scalar1=1.0 / 128, scalar2=EPS,
                                                                       op0=ALU.mult, op1=ALU.add), reads=[W.Bst], writes=[W.Bst])
                                k.use('pool', lambda e: e.tensor_tensor(out=P.rstd[:, 1:3], in0=P.ssq[:, 1:3], in1=P.mh[:, 1:3], op=ALU.pow),
                                      reads=[W.Bst, BC], writes=[W.Bst])
                                k.use('dve', lambda e: e.tensor_scalar(out=lat[:, 0:256], in0=PF[1][:, 256:512], scalar1=P.rstd[:, 1:2],
                                                                       scalar2=None, op0=ALU.mult), reads=[BPF[1], W.Bst], writes=[Blt])
                                k.use('dve', lambda e: e.tensor_scalar(out=lat[:, 256:384], in0=PF[2][:, 0:128], scalar1=P.rstd[:, 2:3],
                                                                       scalar2=None, op0=ALU.mult), reads=[BPF[2], W.Bst], writes=[Blt])
                                k.use('act', lambda e, b=b: e.activation(out=krT[:, b, :], in_=PF[2][:, 128:160], func=AF.Copy),
                                      reads=[BPF[2]], writes=[Bkr])
                                k.group('pe', [lambda e, c=c: e.transpose(out=PT_[1][:, c * 128:(c + 1) * 128], in_=lat[:, c * 128:(c + 1) * 128],
                                                                          identity=P.identb[:]) for c in range(3)],
                                        reads=[Blt, BC], writes=[BPT[1]])
                                k.use('act', lambda e, b=b: e.activation(out=cqnT[:, :, b * 128:(b + 1) * 128],
                                                                         in_=PT_[1][:, 0:256].rearrange("p (a b) -> p a b", a=2),
                                                                         func=AF.Copy), reads=[BPT[1]], writes=[Blat])
                                k.use('act', lambda e, b=b: e.activation(out=ckvnT[:, b * 128:(b + 1) * 128], in_=PT_[1][:, 256:384], func=AF.Copy),
                                      reads=[BPT[1]], writes=[Blat])
                                if b >= 1 and DBG >= 5:
                                    win_attn(b - 1)
                            if DBG >= 5:
                                win_attn(NB - 1)
                            k.barrier()

                    @run_if('p2' in phases)
                    def _p2():
                        with ExitStack() as p2:
                            HP = 2
                            QTb = sbt(p2, "QTb", [128, HP, S], BF16); KTb = sbt(p2, "KTb", [128, HP, S], BF16)
                            VEb = sbt(p2, "VEb", [128, NB, HP, 128], BF16)
                            BQTb = Buf(); BKTb = Buf(); BVEb = Buf()
                            TB = mk_nr_tmp(p2, "nrB", HP * 96, HP * 16)
                            TB.Bcs = Brt
                            kfull = sbt(p2, "kfull", [128, HP, 96]); Bkf = Buf()
                            qbf = sbt(p2, "qb_bf", [128, HP, 128], BF16); kbf = sbt(p2, "kb_bf", [128, HP, 128], BF16)
                            Bqbf = Buf(); Bkbf = Buf()
                            PTm = [sbt(p2, f"PTm{i}", [128, 512], BF16) for i in range(4)]; BPTm = [Buf() for _ in range(4)]
                            rdm = sbt(p2, "rdm", [128, 512]); Brdm = Buf()
                            k.use('pool', lambda e: e.memset(VEb[:, :, :, 64:128], 1.0), writes=[BVEb])
                            k.use('pool', lambda e: e.memset(qbf[:, :, 96:128], 0.0), writes=[Bqbf])
                            k.use('pool', lambda e: e.memset(kbf[:, :, 96:128], 0.0), writes=[Bkbf])
                            gq3b = P.gqb[:].unsqueeze(1).broadcast_to([128, HP, 96])
                            gk3b = P.gkb[:].unsqueeze(1).broadcast_to([128, HP, 96])
                            ptc = [0]
                            for hp in range(8 // HP):
                                h0 = hp * HP
                                for b in range(NB):
                                    tok = slice(b * 128, (b + 1) * 128)
                                    k.group('pe', [lambda e, c=c, tok=tok, h0=h0: e.matmul(
                                        PF[0][:, 0:HP * 96], lhsT=cqnT[:, c, tok], rhs=P.wuq[:, c, h0 * 96:(h0 + HP) * 96],
                                        start=(c == 0), stop=(c == 1)) for c in range(2)], reads=[Blat, BC], writes=[BPF[0]])
                                    k.use('pe', lambda e, tok=tok, h0=h0: e.matmul(
                                        PF[1][:, 0:HP * 128], lhsT=ckvnT[:, tok], rhs=P.wukv[:, h0 * 128:(h0 + HP) * 128],
                                        start=True, stop=True), reads=[Blat, BC], writes=[BPF[1]])
                                    kv3 = PF[1][:, 0:HP * 128].rearrange("p (h d) -> p h d", h=HP)
                                    if DBG < 11:
                                        continue
                                    k.use('act', lambda e, kv3=kv3: e.activation(out=kfull[:, :, 0:64], in_=kv3[:, :, 0:64], func=AF.Copy),
                                          reads=[BPF[1]], writes=[Bkf])
                                    k.use('pool', lambda e, b=b: e.tensor_copy(out=kfull[:, :, 64:96],
                                                                               in_=krT[:, b, :].unsqueeze(1).broadcast_to([128, HP, 32])),
                                          reads=[Bkr], writes=[Bkf])
                                    k.use('act', lambda e, kv3=kv3, b=b: e.activation(out=VEb[:, b, :, 0:64], in_=kv3[:, :, 64:128], func=AF.Copy),
                                          reads=[BPF[1]], writes=[BVEb])
                                    if DBG < 12:
                                        continue
                                    normrope(PF[0][:, 0:HP * 96].rearrange("p (h d) -> p h d", h=HP), [BPF[0]], HP, 96, 64, 16, gq3b,
                                             cosB[:, b, :], sinB[:, b, :], qbf[:, :, 0:96], Bqbf, TB)
                                    normrope(kfull[:], [Bkf], HP, 96, 64, 16, gk3b, cosB[:, b, :], sinB[:, b, :], kbf[:, :, 0:96], Bkbf, TB)
                                    if DBG < 13:
                                        continue
                                    k.group('pe', [lambda e, h=h: e.transpose(out=PT_[1][:, h * 128:(h + 1) * 128], in_=qbf[:, h, :],
                                                                              identity=P.identb[:]) for h in range(HP)] +
                                            [lambda e, h=h: e.transpose(out=PT_[1][:, (HP + h) * 128:(HP + h + 1) * 128], in_=kbf[:, h, :],
                                                                        identity=P.identb[:]) for h in range(HP)],
                                            reads=[Bqbf, Bkbf, BC], writes=[BPT[1]])
                                    if DBG2 < 1:
                                        continue
                                    k.use('act', lambda e, tok=tok: e.activation(out=QTb[:, :, tok],
                                                                                 in_=PT_[1][:, 0:HP * 128].rearrange("p (a b) -> p a b", a=HP),
                                                                                 func=AF.Copy), reads=[BPT[1]], writes=[BQTb])
                                    if DBG2 < 2:
                                        continue
                                    k.use('act', lambda e, tok=tok: e.activation(out=KTb[:, :, tok],
                                                                                 in_=PT_[1][:, HP * 128:2 * HP * 128].rearrange("p (a b) -> p a b", a=HP),
                                                                                 func=AF.Copy), reads=[BPT[1]], writes=[BKTb])
                                if DBG < 14:
                                    continue
                                QW = min(512, S)
                                oc = 0
                                for qt in range(S // QW):
                                    qs = slice(qt * QW, (qt + 1) * QW)
                                    for hh in range(HP):
                                        h = h0 + hh
                                        po = 4 + (oc % 2)
                                        oc += 1
                                        for kb in range(NB):
                                            sp_ = 2 + (ptc[0] % 2)
                                            pt = ptc[0] % 4
                                            ptc[0] += 1
                                            k.use('pe', lambda e, kb=kb, hh=hh, qs=qs, sp_=sp_: e.matmul(
                                                PF[sp_][:, 0:QW], lhsT=KTb[:, hh, kb * 128:(kb + 1) * 128], rhs=QTb[:, hh, qs],
                                                start=True, stop=True), reads=[BKTb, BQTb], writes=[BPF[sp_]])
                                            k.use('act', lambda e, sp_=sp_, pt=pt: e.activation(out=PTm[pt][:, 0:QW], in_=PF[sp_][:, 0:QW], func=AF.Exp),
                                                  reads=[BPF[sp_]], writes=[BPTm[pt]])
                                            if DBG >= 15:
                                                k.use('pe', lambda e, kb=kb, hh=hh, pt=pt, po=po: e.matmul(
                                                    PF[po][:, 0:QW], lhsT=VEb[:, kb, hh, :], rhs=PTm[pt][:, 0:QW],
                                                    start=(kb == 0), stop=(kb == NB - 1)), reads=[BVEb, BPTm[pt]], writes=[BPF[po]])
                                        if DBG < 16:
                                            continue
                                        k.use('dve', lambda e, po=po: e.reciprocal(out=rdm[64:128, 0:QW], in_=PF[po][64:128, 0:QW]),
                                              reads=[BPF[po]], writes=[Brdm])
                                        k.use('dve', lambda e, po=po, h=h, qs=qs: e.tensor_tensor(
                                            out=obT[(h % 2) * 64:(h % 2) * 64 + 64, h // 2, qs], in0=PF[po][0:64, 0:QW],
                                            in1=rdm[64:128, 0:QW], op=ALU.mult), reads=[BPF[po], Brdm], writes=[BobT])
                            k.barrier()
                k.barrier()

                @run_if('p3a' in phases)
                def _p3a():
                    with ExitStack() as p3:
                        TW = min(512, S)
                        NBT = TW // 128
                        wg = sbt(p3, "wg", [128, KC, 2048], BF16); woa = sbt(p3, "woa", [128, 4, D], BF16)
                        wob = sbt(p3, "wob", [128, 4, D], BF16); wo = sbt(p3, "wo", [128, KC, D], BF16)
                        Bw3 = Buf()
                        ws = k.dsem(f"ws3_{s}")
                        load_w('pool', wg, w_in, KC, 1184, 2048, ws, Bw3)
                        load_w('pool', woa, w_o_a, 4, 0, D, ws, Bw3)
                        load_w('pool', wob, w_o_b, 4, 0, D, ws, Bw3)
                        load_w('pool', wo, w_out, KC, 0, D, ws, Bw3)
                        W = NS()
                        W.junk = sbt(p3, "junk3", [128, D], BF16); W.xn = sbt(p3, "xn3", [128, D], BF16)
                        W.Bjunk = Buf(); W.Bst = Buf(); W.Bxn = Buf()
                        xts = [sbt(p3, f"x3_{i}", [128, D]) for i in range(NBT)]; Bxts = [Buf() for _ in range(NBT)]
                        xsem = [k.dsem(f"x3_{s}_{i}") for i in range(NBT)]
                        hT = sbt(p3, "hT3", [128, KC, TW], BF16); BhT = Buf()
                        ta = [sbt(p3, "ta0", [128, TW])] * 2; tb = [sbt(p3, "tb0", [128, TW])] * 2
                        Bta = [Buf()] * 2; Btb = [Buf()] * 2
                        m1 = [sbt(p3, "m1_0", [128, TW])] * 2; m2 = [sbt(p3, "m2_0", [128, TW])] * 2
                        Bm1 = [Buf()] * 2; Bm2 = [Buf()] * 2
                        mixT = sbt(p3, "mixT", [128, KC, TW], BF16); BmixT = Buf()
                        r1 = [sbt(p3, "r1_0", [128, D])] * 2; Br1 = [Buf()] * 2
                        osem = [k.dsem(f"o3_{s}_{i}") for i in range(2)]
                        By = Buf("y")
                        oc = 0
                        for t in range(S // TW):
                            ts_ = slice(t * TW, (t + 1) * TW)
                            par = t % 2
                            for jb in range(NBT):
                                xi = jb
                                r0 = t * TW + jb * 128
                                k.dma_group('sp', [(xts[xi][:], x[s, r0:r0 + 128, :])], xsem[xi], writes=[Bxts[xi]])
                                prep(xts[xi], Bxts[xi], 128, P.a1, P.s1, hT, BhT, jb * 128, W)
                            for m in range(KC):
                                mp = m % 2
                                k.group('pe', [lambda e, kc=kc, m=m: e.matmul(PF[0][:, 0:TW], lhsT=wg[:, kc, m * 128:(m + 1) * 128],
                                                                              rhs=hT[:, kc, :], start=(kc == 0), stop=(kc == KC - 1))
                                               for kc in range(KC)], reads=[Bw3, BhT], writes=[BPF[0]])
                                k.use('act', lambda e, mp=mp: e.activation(out=ta[mp][:], in_=PF[0][:, 0:TW], func=AF.Tanh, scale=0.5),
                                      reads=[BPF[0]], writes=[Bta[mp]])
                                k.group('pe', [lambda e, kc=kc, m=m: e.matmul(PF[1][:, 0:TW], lhsT=wg[:, kc, D + m * 128:D + (m + 1) * 128],
                                                                              rhs=hT[:, kc, :], start=(kc == 0), stop=(kc == KC - 1))
                                               for kc in range(KC)], reads=[Bw3, BhT], writes=[BPF[1]])
                                k.use('act', lambda e, mp=mp: e.activation(out=tb[mp][:], in_=PF[1][:, 0:TW], func=AF.Tanh, scale=0.5),
                                      reads=[BPF[1]], writes=[Btb[mp]])
                                k.group('pe', [lambda e, c=c, m=m, ts_=ts_: e.matmul(PF[2][:, 0:TW], lhsT=woa[:, c, m * 128:(m + 1) * 128],
                                                                                     rhs=oaT[:, c, ts_], start=(c == 0), stop=(c == 3))
                                               for c in range(4)], reads=[Bw3, BoaT], writes=[BPF[2]])
                                k.group('pe', [lambda e, c=c, m=m, ts_=ts_: e.matmul(PF[3][:, 0:TW], lhsT=wob[:, c, m * 128:(m + 1) * 128],
                                                                                     rhs=obT[:, c, ts_], start=(c == 0), stop=(c == 3))
                                               for c in range(4)], reads=[Bw3, BobT], writes=[BPF[3]])
                                k.use('dve', lambda e, mp=mp: e.scalar_tensor_tensor(out=m1[mp][:], in0=ta[mp][:], scalar=1.0, in1=PF[2][:, 0:TW],
                                                                                     op0=ALU.add, op1=ALU.mult), reads=[Bta[mp], BPF[2]], writes=[Bm1[mp]])
                                k.use('dve', lambda e, mp=mp: e.scalar_tensor_tensor(out=m2[mp][:], in0=tb[mp][:], scalar=1.0, in1=PF[3][:, 0:TW],
                                                                                     op0=ALU.add, op1=ALU.mult), reads=[Btb[mp], BPF[3]], writes=[Bm2[mp]])
                                k.use('pool', lambda e, mp=mp, m=m: e.tensor_tensor(out=mixT[:, m, :], in0=m1[mp][:], in1=m2[mp][:], op=ALU.add),
                                      reads=[Bm1[mp], Bm2[mp]], writes=[BmixT])
                            for jb in range(NBT):
                                xi = jb
                                r0 = t * TW + jb * 128
                                rs_ = oc % 2
                                oc += 1
                                for hf in range(2):
                                    k.group('pe', [lambda e, kc=kc, jb=jb, hf=hf: e.matmul(PF[4 + hf][:, :], lhsT=mixT[:, kc, jb * 128:(jb + 1) * 128],
                                                                                          rhs=wo[:, kc, hf * 512:(hf + 1) * 512],
                                                                                          start=(kc == 0), stop=(kc == KC - 1)) for kc in range(KC)],
                                            reads=[BmixT, Bw3], writes=[BPF[4 + hf]])
                                    k.use('dve', lambda e, hf=hf, rs_=rs_: e.tensor_tensor(out=r1[rs_][:, hf * 512:(hf + 1) * 512], in0=PF[4 + hf][:, :],
                                                                                           in1=P.gtbc[:, hf * 512:(hf + 1) * 512], op=ALU.mult),
                                          reads=[BPF[4 + hf], Bgt], writes=[Br1[rs_]])
                                    k.use('pool', lambda e, hf=hf, rs_=rs_, xi=xi: e.tensor_tensor(out=r1[rs_][:, hf * 512:(hf + 1) * 512],
                                                                                                   in0=r1[rs_][:, hf * 512:(hf + 1) * 512],
                                                                                                   in1=xts[xi][:, hf * 512:(hf + 1) * 512], op=ALU.add),
                                          reads=[Bxts[xi]], writes=[Br1[rs_]])
                                k.dma_group('sp', [(x1d[s, r0:r0 + 128, :], r1[rs_][:])], osem[rs_], reads=[Br1[rs_]], writes=[By])
                        k.barrier()
            k.barrier()

            @run_if('p3b' in phases)
            def _p3b():
                with ExitStack() as p4:
                    wup = sbt(p4, "wup", [128, KC, 2 * DFF], BF16); wdn = sbt(p4, "wdn", [128, NFF, D], BF16)
                    Bw4 = Buf()
                    ws = k.dsem(f"ws4_{s}")
                    load_w('pool', wup, w_up, KC, 0, 2 * DFF, ws, Bw4)
                    load_w('pool', wdn, w_down, NFF, 0, D, ws, Bw4)
                    W = NS()
                    W.junk = sbt(p4, "junk4", [128, D], BF16); W.xn = sbt(p4, "xn4", [128, D], BF16)
                    W.Bjunk = Buf(); W.Bst = Buf(); W.Bxn = Buf()
                    NBF = FT // 128
                    xts = [sbt(p4, f"x4_{i}", [128, D]) for i in range(NBF)]; Bxts = [Buf() for _ in range(NBF)]
                    xsem = [k.dsem(f"x4_{s}_{i}") for i in range(NBF)]
                    hT = sbt(p4, "hT4", [128, KC, FT], BF16); BhT = Buf()
                    ua = [sbt(p4, f"ua{i}", [128, FT]) for i in range(2)]; ug = [sbt(p4, f"ug{i}", [128, FT]) for i in range(2)]
                    Bua = [Buf(), Buf()]; Bug = [Buf(), Buf()]
                    tg = [sbt(p4, f"tg{i}", [128, FT]) for i in range(2)]; Btg = [Buf(), Buf()]
                    actT = sbt(p4, "actT", [128, NFF, FT], BF16); BactT = Buf()
                    yo = [sbt(p4, f"yo{i}", [128, D]) for i in range(2)]; Byo = [Buf(), Buf()]
                    osem = [k.dsem(f"o4_{s}_{i}") for i in range(2)]
                    k.use('pool', lambda e: e.memset(actT[:], 0.0), writes=[BactT])
                    By = Buf("y4")
                    cw3 = P.convw[:].rearrange("p (j m) -> p j m", j=3)
                    oc = 0
                    ocn = [0]

                    def do_tile(lo, hi, vs, ve):
                        n = hi - lo
                        v0 = vs - lo
                        v1 = ve - lo
                        blocks = []
                        r0 = lo
                        while r0 < hi:
                            nr = min(128, hi - r0)
                            blocks.append((r0, nr))
                            r0 += nr
                        for bi, (r0, nr) in enumerate(blocks):
                            k.dma_group('sp', [(xts[bi][0:nr, :], x1d[s, r0:r0 + nr, :])], xsem[bi], writes=[Bxts[bi]])
                            prep(xts[bi], Bxts[bi], nr, P.a2, P.s2, hT, BhT, r0 - lo, W)
                        for j in range(NFF):
                            jp = j % 2
                            for which, (uu, Buu) in enumerate(((ua, Bua), (ug, Bug))):
                                col = which * DFF + j * 128
                                m = which * NFF + j
                                pz = PF[which * 2 + jp]
                                Bpz = BPF[which * 2 + jp]
                                k.group('pe', [lambda e, kc=kc, col=col, pz=pz: e.matmul(pz[:, 0:n], lhsT=wup[:, kc, col:col + 128], rhs=hT[:, kc, 0:n],
                                                                                         start=(kc == 0), stop=(kc == KC - 1)) for kc in range(KC)],
                                        reads=[Bw4, BhT], writes=[Bpz])
                                u_ = uu[jp]
                                k.use('act', lambda e, u_=u_, pz=pz, m=m: e.activation(out=u_[:, v0:v1], in_=pz[:, v0:v1], func=AF.Identity,
                                                                                       scale=cw3[:, 1, m:m + 1], bias=P.convb[:, m:m + 1]),
                                      reads=[Bpz, BC], writes=[Buu[jp]])
                                la = max(v0, 1)
                                k.use('dve', lambda e, u_=u_, pz=pz, m=m, la=la: e.scalar_tensor_tensor(
                                    out=u_[:, la:v1], in0=pz[:, la - 1:v1 - 1], scalar=cw3[:, 0, m:m + 1], in1=u_[:, la:v1],
                                    op0=ALU.mult, op1=ALU.add), reads=[Bpz, BC], writes=[Buu[jp]])
                                rb = min(v1, n - 1)
                                k.use('dve', lambda e, u_=u_, pz=pz, m=m, rb=rb: e.scalar_tensor_tensor(
                                    out=u_[:, v0:rb], in0=pz[:, v0 + 1:rb + 1], scalar=cw3[:, 2, m:m + 1], in1=u_[:, v0:rb],
                                    op0=ALU.mult, op1=ALU.add), reads=[Bpz, BC], writes=[Buu[jp]])
                            k.use('act', lambda e, jp=jp: e.activation(out=tg[jp][:, v0:v1], in_=ug[jp][:, v0:v1], func=AF.Tanh, scale=0.5),
                                  reads=[Bug[jp]], writes=[Btg[jp]])
                            k.use('pool', lambda e, jp=jp: e.tensor_tensor(out=ug[jp][:, v0:v1], in0=ug[jp][:, v0:v1], in1=ua[jp][:, v0:v1], op=ALU.mult),
                                  reads=[Bua[jp]], writes=[Bug[jp]])
                            k.use('dve', lambda e, jp=jp, j=j: e.scalar_tensor_tensor(out=actT[:, j, v0:v1], in0=tg[jp][:, v0:v1], scalar=1.0,
                                                                                      in1=ug[jp][:, v0:v1], op0=ALU.add, op1=ALU.mult),
                                  reads=[Btg[jp], Bug[jp]], writes=[BactT])
                        for bi, (r0, nr) in enumerate(blocks):
                            c0 = r0 - lo
                            pa = max(vs, r0) - r0
                            pb = min(ve, r0 + nr) - r0
                            if pb <= pa:
                                continue
                            rs_ = ocn[0] % 2
                            ocn[0] += 1
                            for hf in range(2):
                                k.group('pe', [lambda e, j=j, c0=c0, nr=nr, hf=hf: e.matmul(PF[4 + hf][0:nr, :], lhsT=actT[:, j, c0:c0 + nr],
                                                                                           rhs=wdn[:, j, hf * 512:(hf + 1) * 512],
                                                                                           start=(j == 0), stop=(j == NFF - 1)) for j in range(NFF)],
                                        reads=[BactT, Bw4], writes=[BPF[4 + hf]])
                                k.use('dve', lambda e, hf=hf, rs_=rs_, nr=nr: e.tensor_tensor(out=yo[rs_][0:nr, hf * 512:(hf + 1) * 512], in0=PF[4 + hf][0:nr, :],
                                                                                              in1=P.gtbc[0:nr, D + hf * 512:D + (hf + 1) * 512], op=ALU.mult),
                                      reads=[BPF[4 + hf], Bgt], writes=[Byo[rs_]])
                                k.use('pool', lambda e, hf=hf, rs_=rs_, nr=nr, bi=bi: e.tensor_tensor(out=yo[rs_][0:nr, hf * 512:(hf + 1) * 512],
                                                                                                      in0=yo[rs_][0:nr, hf * 512:(hf + 1) * 512],
                                                                                                      in1=xts[bi][0:nr, hf * 512:(hf + 1) * 512], op=ALU.add),
                                      reads=[Bxts[bi]], writes=[Byo[rs_]])
                            k.dma_group('sp', [(y[s, r0 + pa:r0 + pb, :], yo[rs_][pa:pb, :])], osem[rs_], reads=[Byo[rs_]], writes=[By])

                    for (lo_, hi_, vs_, ve_) in ffn_tiles(S):
                        do_tile(lo_, hi_, vs_, ve_)
                    k.barrier()
        for s_ in range(nseq):
            do_seq(s_)
        k.barrier()
        k.emit()
    return nc


def rope_tables(S, dim):
    inv = (1.0 / (np.float32(10000.0) ** (np.arange(0, dim, 2, dtype=np.float32) / np.float32(dim)))).astype(np.float32)
    ang = (np.arange(S, dtype=np.float32)[:, None] * inv[None, :]).astype(np.float32)
    return np.cos(ang).astype(np.float32), np.sin(ang).astype(np.float32)


def cols(v, nch):
    return np.ascontiguousarray(np.asarray(v, np.float32).reshape(nch, 128).T)


def make_in_maps(seqs_per_core, cs_per_core, weights, S, nseq):
    NB = S // 128
    w = {kk: np.asarray(v, np.float32)[0] for kk, v in weights.items()}
    cosA, sinA = rope_tables(S, 64)
    cosB, sinB = rope_tables(S, 32)

    def tab(t):
        R = t.shape[1]
        return np.ascontiguousarray(t.reshape(NB, 128, R).transpose(1, 0, 2).reshape(128, NB * R))
    kk_ = np.arange(128)[:, None]
    qq_ = np.arange(128)[None, :]
    sel = np.zeros((nseq, nseq, 128), np.float32)
    for i in range(nseq):
        sel[i, i, :] = 1.0
    shared = {
        "w_ada": np.ascontiguousarray(w['w_ada']),
        "b_ada_rep": np.ascontiguousarray(np.broadcast_to(w['b_ada'][None, :], (nseq, 6 * D))),
        "g_attn_c": cols(w['g_attn'], 8), "g_ffn_c": cols(w['g_ffn'], 8),
        "w_in": np.ascontiguousarray(w['w_in']),
        "gA": np.ascontiguousarray(np.broadcast_to(np.concatenate([np.tile(w['gq_a'], 8), np.tile(w['gk_a'], 2)])[None, :], (128, 640))),
        "gqb": np.ascontiguousarray(np.broadcast_to(w['gq_b'][None, :], (128, 96))),
        "gkb": np.ascontiguousarray(np.broadcast_to(w['gk_b'][None, :], (128, 96))),
        "sink_rep": np.ascontiguousarray(np.broadcast_to(w['sink_a'][None, :], (128, 8))),
        "g_cq_c": cols(w['g_cq'], 2), "g_ckv_c": cols(w['g_ckv'], 1),
        "w_uq": np.ascontiguousarray(w['w_uq']), "w_ukv": np.ascontiguousarray(w['w_ukv']),
        "w_o_a": np.ascontiguousarray(w['w_o_a']), "w_o_b": np.ascontiguousarray(w['w_o_b']),
        "w_out": np.ascontiguousarray(w['w_out']), "w_up": np.ascontiguousarray(w['w_up']),
        "w_down": np.ascontiguousarray(w['w_down']),
        "convw_c": np.ascontiguousarray(np.concatenate([cols(w['conv_w'][j], 44) for j in range(3)], axis=1)),
        "convb_c": cols(w['conv_b'], 44),
        "ident": np.eye(128, dtype=np.float32),
        "maskL": (kk_ >= qq_).astype(np.float32), "maskR": (kk_ <= qq_).astype(np.float32),
        "cosA": tab(cosA), "sinA": tab(sinA), "cosB": tab(cosB), "sinB": tab(sinB),
        "sel": np.ascontiguousarray(sel.reshape(nseq, nseq * 128)),
    }
    in_maps = []
    for xs, cs in zip(seqs_per_core, cs_per_core):
        m = dict(shared)
        m["x"] = np.ascontiguousarray(xs, dtype=np.float32)
        cs = np.asarray(cs, np.float32)
        m["cT"] = np.ascontiguousarray(cs.reshape(nseq, KC, 128).transpose(2, 1, 0).reshape(128, KC * nseq))
        in_maps.append(m)
    return in_maps


WEIGHT_NAMES = ['w_ada', 'b_ada', 'g_attn', 'w_in', 'gq_a', 'gk_a', 'sink_a', 'g_cq', 'w_uq', 'g_ckv', 'w_ukv', 'gq_b', 'gk_b',
                'w_o_a', 'w_o_b', 'w_out', 'g_ffn', 'w_up', 'conv_w', 'conv_b', 'w_down']


def run_layout(seqs_per_core, cs_per_core, weights, S, nseq, phases=('p1', 'p2', 'p3a', 'p3b')):
    nc = build_program(nseq, S, phases)
    in_maps = make_in_maps(seqs_per_core, cs_per_core, weights, S, nseq)
    if os.environ.get('KTRACE'):
        res = run_bass_kernel_spmd(nc, in_maps, core_ids=list(range(N_CORES)), trace=True)
        print("EXEC_TIME_NS", res.exec_time_ns, flush=True)
    else:
        res = run_bass_kernel_spmd(nc, in_maps, core_ids=list(range(N_CORES)))
    return [r["y"] for r in res.results]


def kernel(x_prompt, x_sample, c_prompt, c_sample, **weights):
    x_prompt = np.asarray(x_prompt, np.float32)
    x_sample = np.asarray(x_sample, np.float32)
    c_prompt = np.asarray(c_prompt, np.float32)
    c_sample = np.asarray(c_sample, np.float32)
    S = x_prompt.shape[1]
    seqs, cs = [], []
    for i in range(N_CORES):
        seqs.append(np.stack([x_prompt[i], x_sample[2 * i], x_sample[2 * i + 1]], 0))
        cs.append(np.stack([c_prompt[i], c_sample[2 * i], c_sample[2 * i + 1]], 0))
    ys = run_layout(seqs, cs, {n: weights[n] for n in WEIGHT_NAMES}, S, 3)
    y_prompt = np.stack([ys[i][0] for i in range(N_CORES)], 0)
    y_sample = np.stack([ys[i][1 + j] for i in range(N_CORES) for j in range(2)], 0)
    return (y_prompt, y_sample)
```
